# Optimizing a Trainium2 kernel written in Bass

```python
import math
import jax, jax.numpy as jnp
from jax import lax
import numpy as np

D_MODEL = 2048
BATCH = 4
SEQ = 2048
DEPTH = 4

GRID_W = 64
CTX_LEN = 256
ATTN_HEADS = 8
ATTN_KV_HEADS = 2
GQA_GROUP = ATTN_HEADS // ATTN_KV_HEADS
HEAD_DIM = 128
ATTN_WIDTH = ATTN_HEADS * HEAD_DIM
KV_WIDTH = ATTN_KV_HEADS * HEAD_DIM
Q_BLOCK = 128
ROPE_THETA = 10000.0
ROPE_PAIRS = HEAD_DIM // 4
SSD_INNER = D_MODEL // 2
SSD_HEAD_DIM = 64
SSD_HEADS = SSD_INNER // SSD_HEAD_DIM
SSD_GROUPS = 2
SSD_HEADS_PER_GROUP = SSD_HEADS // SSD_GROUPS
SSD_STATE = 128
SSD_CONV = 3
SSD_CHUNK = 128
XBC_WIDTH = SSD_INNER + 2 * SSD_GROUPS * SSD_STATE
MIX_WIDTH = ATTN_WIDTH + SSD_INNER
IN_COLS = ATTN_WIDTH + 2 * KV_WIDTH + SSD_INNER + XBC_WIDTH + 2 * SSD_HEADS
D_FF = 5632
FFN_CONV = 3
N_MOD = 6
EPS = 1e-6

kernel_name = "hymba_gqa_ssd_convffn_prefix_dit"


def rms_norm(x, w):
    xf = x.astype(jnp.float32)
    y = xf * lax.rsqrt(jnp.mean(xf * xf, axis=-1, keepdims=True) + EPS)
    return (y * w.astype(jnp.float32)).astype(x.dtype)


def depthwise_conv(x, w, b):
    k = w.shape[0]
    y = lax.conv_general_dilated(x, w[:, None, :], window_strides=(1,),
                                 padding=((k // 2, k // 2),),
                                 dimension_numbers=('NWC', 'WIO', 'NWC'),
                                 feature_group_count=x.shape[-1])
    return y + b


def axial_rope_tables(seq_len):
    rows = seq_len // GRID_W
    row, col = jnp.meshgrid(jnp.arange(rows), jnp.arange(GRID_W), indexing='ij')
    inv_freq = ROPE_THETA ** (-jnp.arange(ROPE_PAIRS, dtype=jnp.float32) / ROPE_PAIRS)
    ang_r = row.reshape(-1).astype(jnp.float32)[:, None] * inv_freq
    ang_c = col.reshape(-1).astype(jnp.float32)[:, None] * inv_freq
    return (jnp.cos(ang_r), jnp.sin(ang_r), jnp.cos(ang_c), jnp.sin(ang_c))


def rope_half(x, cos, sin):
    x1, x2 = jnp.split(x, 2, axis=-1)
    c = cos[:, None, :]
    s = sin[:, None, :]
    return jnp.concatenate([x1 * c - x2 * s, x2 * c + x1 * s], axis=-1)


def apply_axial_rope(x, tables):
    cos_r, sin_r, cos_c, sin_c = tables
    x_row, x_col = jnp.split(x, 2, axis=-1)
    return jnp.concatenate([rope_half(x_row, cos_r, sin_r), rope_half(x_col, cos_c, sin_c)], axis=-1).astype(x.dtype)


def modulation(cond, w_ada, b_ada):
    return (jax.nn.silu(cond) @ w_ada + b_ada)[:, None, :]


def split_projection(p):
    bsz, t = p.shape[:2]
    cuts = [ATTN_WIDTH, ATTN_WIDTH + KV_WIDTH, ATTN_WIDTH + 2 * KV_WIDTH,
            ATTN_WIDTH + 2 * KV_WIDTH + SSD_INNER,
            ATTN_WIDTH + 2 * KV_WIDTH + SSD_INNER + XBC_WIDTH]
    q, k, v, z, xbc, dt = jnp.split(p, cuts, axis=-1)
    q = q.reshape(bsz, t, ATTN_HEADS, HEAD_DIM)
    k = k.reshape(bsz, t, ATTN_KV_HEADS, HEAD_DIM)
    v = v.reshape(bsz, t, ATTN_KV_HEADS, HEAD_DIM)
    return q, k, v, z, xbc, dt


def grouped_sdpa(q, k, v):
    s = jnp.einsum('bqkgd,bskd->bkgqs', q, k).astype(jnp.float32) * (HEAD_DIM ** -0.5)
    p = jax.nn.softmax(s, axis=-1).astype(v.dtype)
    return jnp.einsum('bkgqs,bskd->bqkgd', p, v)


def attention_mixer(q_l, k_l, v_l, q_c, k_c, v_c, rope, q_norm_w, k_norm_w, need_ctx_out):
    bsz, seq_len = q_l.shape[:2]
    ctx_len = q_c.shape[1]
    q_l = apply_axial_rope(rms_norm(q_l, q_norm_w), rope)
    k_l = apply_axial_rope(rms_norm(k_l, k_norm_w), rope)
    q_c = rms_norm(q_c, q_norm_w)
    k_c = rms_norm(k_c, k_norm_w)
    k_all = jnp.concatenate([k_c, k_l], axis=1)
    v_all = jnp.concatenate([v_c, v_l], axis=1)
    n_blk = seq_len // Q_BLOCK
    q_blocks = q_l.reshape(bsz, n_blk, Q_BLOCK, ATTN_KV_HEADS, GQA_GROUP, HEAD_DIM).swapaxes(0, 1)
    o_l = lax.map(lambda qb: grouped_sdpa(qb, k_all, v_all), q_blocks)
    o_l = o_l.swapaxes(0, 1).reshape(bsz, seq_len, ATTN_WIDTH)
    o_c = None
    if need_ctx_out:
        q_cg = q_c.reshape(bsz, ctx_len, ATTN_KV_HEADS, GQA_GROUP, HEAD_DIM)
        o_c = grouped_sdpa(q_cg, k_c, v_c).reshape(bsz, ctx_len, ATTN_WIDTH)
    return o_l, o_c


def ssd_chunked(xs, dt, a, bm, cm, init_state):
    bsz, t = xs.shape[:2]
    nc = t // SSD_CHUNK
    e = SSD_HEADS_PER_GROUP
    xdt = (xs.astype(jnp.float32) * dt[..., None]).reshape(bsz, nc, SSD_CHUNK, SSD_GROUPS, e, SSD_HEAD_DIM)
    da = (dt * a).reshape(bsz, nc, SSD_CHUNK, SSD_GROUPS, e)
    bm = bm.astype(jnp.float32).reshape(bsz, nc, SSD_CHUNK, SSD_GROUPS, SSD_STATE)
    cm = cm.astype(jnp.float32).reshape(bsz, nc, SSD_CHUNK, SSD_GROUPS, SSD_STATE)
    cs = jnp.cumsum(da, axis=2)
    causal = jnp.tril(jnp.ones((SSD_CHUNK, SSD_CHUNK), dtype=bool))[:, :, None, None]
    seg = cs[:, :, :, None] - cs[:, :, None, :]
    decay = jnp.exp(jnp.where(causal, seg, -jnp.inf))
    cb = jnp.einsum('bclgn,bcsgn->bclsg', cm, bm)
    y_diag = jnp.einsum('bclsg,bclsge,bcsgep->bclgep', cb, decay, xdt)
    to_end = jnp.exp(cs[:, :, -1:] - cs)
    chunk_states = jnp.einsum('bclgn,bclge,bclgep->bcgepn', bm, to_end, xdt)
    chunk_decay = jnp.exp(cs[:, :, -1])

    def carry_step(state, inp):
        st, dc = inp
        return state * dc[..., None, None] + st, state

    final_state, prev = lax.scan(carry_step, init_state,
                                 (jnp.moveaxis(chunk_states, 1, 0), jnp.moveaxis(chunk_decay, 1, 0)))
    prev = jnp.moveaxis(prev, 0, 1)
    y_off = jnp.einsum('bclgn,bcgepn,bclge->bclgep', cm, prev, jnp.exp(cs))
    y = (y_diag + y_off).reshape(bsz, t, SSD_HEADS, SSD_HEAD_DIM)
    return y, final_state


def ssd_mixer(z_l, xbc_l, dtr_l, z_c, xbc_c, dtr_c, conv_w, conv_b, dt_bias, a_log, d_skip, norm_w, need_ctx_out):
    a = -jnp.exp(a_log.astype(jnp.float32))

    def prep(xbc, dt_raw):
        xbc = jax.nn.silu(depthwise_conv(xbc, conv_w, conv_b))
        xs, bm, cm = jnp.split(xbc, [SSD_INNER, SSD_INNER + SSD_GROUPS * SSD_STATE], axis=-1)
        bsz, t = xs.shape[:2]
        xs = xs.reshape(bsz, t, SSD_HEADS, SSD_HEAD_DIM)
        bm = bm.reshape(bsz, t, SSD_GROUPS, SSD_STATE)
        cm = cm.reshape(bsz, t, SSD_GROUPS, SSD_STATE)
        dt = jax.nn.softplus(dt_raw.astype(jnp.float32).reshape(bsz, t, 2, SSD_HEADS) + dt_bias.astype(jnp.float32))
        return xs, bm, cm, dt

    xs_c, b_c, c_c, dt_c = prep(xbc_c, dtr_c)
    xs_l, b_l, c_l, dt_l = prep(xbc_l, dtr_l)
    bsz = xs_l.shape[0]
    flip = lambda u: jnp.flip(u, axis=1)
    zero = jnp.zeros((bsz, SSD_GROUPS, SSD_HEADS_PER_GROUP, SSD_HEAD_DIM, SSD_STATE), jnp.float32)
    y_cf, s_f = ssd_chunked(xs_c, dt_c[:, :, 0], a[0], b_c, c_c, zero)
    y_lf, _ = ssd_chunked(xs_l, dt_l[:, :, 0], a[0], b_l, c_l, s_f)
    y_cb, s_b = ssd_chunked(flip(xs_c), flip(dt_c[:, :, 1]), a[1], flip(b_c), flip(c_c), zero)
    y_lb, _ = ssd_chunked(flip(xs_l), flip(dt_l[:, :, 1]), a[1], flip(b_l), flip(c_l), s_b)

    def finish(y_f, y_b, xs, z):
        bsz_, t = xs.shape[:2]
        y = y_f + flip(y_b) + d_skip.astype(jnp.float32)[:, None] * xs.astype(jnp.float32)
        y = y.reshape(bsz_, t, SSD_INNER) * jax.nn.silu(z.astype(jnp.float32))
        return rms_norm(y, norm_w).astype(z.dtype)

    o_l = finish(y_lf, y_lb, xs_l, z_l)
    o_c = finish(y_cf, y_cb, xs_c, z_c) if need_ctx_out else None
    return o_l, o_c


def conv_ffn(h, w_up, conv_w, conv_b, w_down):
    u = depthwise_conv(h @ w_up, conv_w, conv_b)
    gate, val = jnp.split(u, 2, axis=-1)
    return (jax.nn.silu(gate) * val) @ w_down


def setup_inputs(seed: int = 0) -> dict:
    key = jax.random.key(seed)
    ks = jax.random.split(key, 24)
    f32 = jnp.float32

    def nrm(k, shape, scale):
        return jax.random.normal(k, shape, f32) * scale

    x = nrm(ks[0], (BATCH, SEQ, D_MODEL), 1.0)
    c = nrm(ks[1], (BATCH, D_MODEL), 1.0)
    ctx = nrm(ks[2], (BATCH, CTX_LEN, D_MODEL), 1.0)
    c_ctx = nrm(ks[3], (D_MODEL,), 1.0)
    w_ada = nrm(ks[4], (DEPTH, D_MODEL, N_MOD * D_MODEL), 0.5 * D_MODEL ** -0.5)
    b_ada = nrm(ks[5], (DEPTH, N_MOD * D_MODEL), 0.01)
    norm_mix_w = 1.0 + nrm(ks[6], (DEPTH, D_MODEL), 0.02)
    w_in_main = nrm(ks[7], (DEPTH, D_MODEL, IN_COLS - 2 * SSD_HEADS), D_MODEL ** -0.5)
    w_in_dt = nrm(ks[8], (DEPTH, D_MODEL, 2 * SSD_HEADS), 0.1 * D_MODEL ** -0.5)
    w_in = jnp.concatenate([w_in_main, w_in_dt], axis=-1)
    q_norm_w = 1.0 + nrm(ks[9], (DEPTH, HEAD_DIM), 0.02)
    k_norm_w = 1.0 + nrm(ks[10], (DEPTH, HEAD_DIM), 0.02)
    ssd_conv_w = nrm(ks[11], (DEPTH, SSD_CONV, XBC_WIDTH), SSD_CONV ** -0.5)
    ssd_conv_b = nrm(ks[12], (DEPTH, XBC_WIDTH), 0.01)
    dt0 = jnp.exp(jax.random.uniform(ks[13], (DEPTH, 2, SSD_HEADS), f32, math.log(1e-3), math.log(1e-1)))
    dt_bias = dt0 + jnp.log(-jnp.expm1(-dt0))
    a_log = jnp.log(jax.random.uniform(ks[14], (DEPTH, 2, SSD_HEADS), f32, 1.0, 16.0))
    d_skip = 1.0 + nrm(ks[15], (DEPTH, SSD_HEADS), 0.02)
    ssd_norm_w = 1.0 + nrm(ks[16], (DEPTH, SSD_INNER), 0.02)
    w_out = nrm(ks[17], (DEPTH, MIX_WIDTH, D_MODEL), MIX_WIDTH ** -0.5)
    norm_mlp_w = 1.0 + nrm(ks[18], (DEPTH, D_MODEL), 0.02)
    w_up = nrm(ks[19], (DEPTH, D_MODEL, 2 * D_FF), D_MODEL ** -0.5)
    ffn_conv_w = nrm(ks[20], (DEPTH, FFN_CONV, 2 * D_FF), FFN_CONV ** -0.5)
    ffn_conv_b = nrm(ks[21], (DEPTH, 2 * D_FF), 0.01)
    w_down = nrm(ks[22], (DEPTH, D_FF, D_MODEL), D_FF ** -0.5)
    return {"x": x, "c": c, "ctx": ctx, "c_ctx": c_ctx,
            "w_ada": w_ada, "b_ada": b_ada, "norm_mix_w": norm_mix_w, "w_in": w_in,
            "q_norm_w": q_norm_w, "k_norm_w": k_norm_w,
            "ssd_conv_w": ssd_conv_w, "ssd_conv_b": ssd_conv_b, "dt_bias": dt_bias, "a_log": a_log,
            "d_skip": d_skip, "ssd_norm_w": ssd_norm_w, "w_out": w_out, "norm_mlp_w": norm_mlp_w,
            "w_up": w_up, "ffn_conv_w": ffn_conv_w, "ffn_conv_b": ffn_conv_b, "w_down": w_down}


def reference(x, c, ctx, c_ctx, w_ada, b_ada, norm_mix_w, w_in, q_norm_w, k_norm_w,
              ssd_conv_w, ssd_conv_b, dt_bias, a_log, d_skip, ssd_norm_w, w_out, norm_mlp_w,
              w_up, ffn_conv_w, ffn_conv_b, w_down):
    seq_len = x.shape[1]
    rope = axial_rope_tables(seq_len)
    for i in range(DEPTH):
        need_ctx = i < DEPTH - 1
        mod_l = modulation(c, w_ada[i], b_ada[i])
        mod_c = modulation(c_ctx[None, :], w_ada[i], b_ada[i])
        sh1_l, sc1_l, g1_l, sh2_l, sc2_l, g2_l = jnp.split(mod_l, N_MOD, axis=-1)
        sh1_c, sc1_c, g1_c, sh2_c, sc2_c, g2_c = jnp.split(mod_c, N_MOD, axis=-1)

        h_l = rms_norm(x, norm_mix_w[i]) * (1.0 + sc1_l) + sh1_l
        h_c = rms_norm(ctx, norm_mix_w[i]) * (1.0 + sc1_c) + sh1_c
        q_l, k_l, v_l, z_l, xbc_l, dtr_l = split_projection(h_l @ w_in[i])
        q_c, k_c, v_c, z_c, xbc_c, dtr_c = split_projection(h_c @ w_in[i])
        att_l, att_c = attention_mixer(q_l, k_l, v_l, q_c, k_c, v_c, rope, q_norm_w[i], k_norm_w[i], need_ctx)
        ssd_l, ssd_c = ssd_mixer(z_l, xbc_l, dtr_l, z_c, xbc_c, dtr_c, ssd_conv_w[i], ssd_conv_b[i],
                                 dt_bias[i], a_log[i], d_skip[i], ssd_norm_w[i], need_ctx)
        x = x + g1_l * (jnp.concatenate([att_l, ssd_l], axis=-1) @ w_out[i])
        h2_l = rms_norm(x, norm_mlp_w[i]) * (1.0 + sc2_l) + sh2_l
        x = x + g2_l * conv_ffn(h2_l, w_up[i], ffn_conv_w[i], ffn_conv_b[i], w_down[i])

        if need_ctx:
            ctx = ctx + g1_c * (jnp.concatenate([att_c, ssd_c], axis=-1) @ w_out[i])
            h2_c = rms_norm(ctx, norm_mlp_w[i]) * (1.0 + sc2_c) + sh2_c
            ctx = ctx + g2_c * conv_ffn(h2_c, w_up[i], ffn_conv_w[i], ffn_conv_b[i], w_down[i])
    return x
```

```python
import numpy as np
from contextlib import ExitStack
import concourse.bass as bass
import concourse.mybir as mybir
from concourse.bass_utils import run_bass_kernel_spmd

F32 = mybir.dt.float32
BF16 = mybir.dt.bfloat16
AF = mybir.ActivationFunctionType
ALU = mybir.AluOpType

ENGS = ["pe", "act", "dve", "pool", "sp"]

NL = 4
D = 2048
KC = 16
TCX = 128
TLAT = 1024
T = 1152
TH = 1156
TT = [(0, 128), (128, 512), (640, 512)]
DFF = 5632
NJ = 44
EPS = 1e-6
PAIRS = [[0, 1], [2, 3], [4, 5], [6, 7]]


class TK:
    __slots__ = ("name", "lw", "rd", "dsem", "dcnt", "owner")

    def __init__(self, name, owner=None):
        self.name = name
        self.lw = []
        self.rd = []
        self.dsem = None
        self.dcnt = 0
        self.owner = owner or self


class Rec:
    def __init__(self):
        self.call = None

    def __getattr__(self, name):
        def f(*a, **k):
            self.call = (name, a, k)
            return self
        return f


class Prog:
    def __init__(self, nc, stack):
        self.nc = nc
        self.stack = stack
        self.ops = {e: [] for e in ENGS}
        self.sem = {e: stack.enter_context(nc.semaphore("s_" + e)) for e in ENGS}
        self.sigbase = {e: 0 for e in ENGS}
        self.seen_e = {e: {} for e in ENGS}
        self.seen_d = {e: {} for e in ENGS}
        self.alltiles = []
        self.cache = {}
        self.owners = []
        self.nops = 0

    def tile(self, name=None, owner=None):
        if name is not None and name in self.cache:
            return self.cache[name]
        t = TK(name or f"t{len(self.alltiles)}", owner)
        self.alltiles.append(t)
        if name is not None:
            self.cache[name] = t
        return t

    def tiles(self, n, name="t", owner=None):
        return [self.tile(f"{name}{i}", owner) for i in range(n)]

    def _deps(self, eng, r, w, tok):
        deps = []
        for t in r:
            deps += t.lw
        for t in w:
            deps += t.lw
            deps += t.rd
        for t in r:
            t.rd.append(tok)
        for t in w:
            t.lw = [tok]
            t.rd = []
        out = []
        for d in deps:
            if d is tok:
                continue
            if d[0] == "E":
                if d[1] == eng and eng == "pe":
                    continue
                d[3] = True
                out.append(d)
            else:
                out.append(["D", d[1], d[1].dcnt])
        return out

    def _rec(self, fn):
        rec = Rec()
        fn(rec)
        assert rec.call is not None
        return rec.call

    def op(self, eng, fn, r=(), w=()):
        call = self._rec(fn)
        lst = self.ops[eng]
        tok = ["E", eng, len(lst), False, None]
        deps = self._deps(eng, r, w, tok)
        lst.append(dict(call=call, deps=deps, tok=tok, inc=None))
        self.nops += 1
        return tok

    def _own(self, st, cls):
        if not hasattr(self, "cls_owners"):
            self.cls_owners = {"sp": [], "pool": [], "cc": []}
            self.phase_map = {}
            self.cls_next = {"sp": 0, "pool": 0, "cc": 0}
        key = (id(st.owner), cls)
        if key in self.phase_map:
            return self.phase_map[key]
        lst = self.cls_owners[cls]
        idx = self.cls_next[cls]
        self.cls_next[cls] += 1
        if idx >= len(lst):
            o = TK(f"{cls}{len(lst)}")
            o.dsem = self.stack.enter_context(self.nc.semaphore("d_" + o.name))
            lst.append(o)
            self.owners.append(o)
        o = lst[idx]
        self.phase_map[key] = o
        return o

    def dma(self, q, out, in_, r=(), w=(), sem_tile=None):
        o = self._own(sem_tile or (w[0] if w else r[0]), "pool" if q == "pool" else "sp")
        tok = ["D", o, o.dcnt + 16]
        deps = self._deps(q, r, w, tok)
        o.dcnt += 16
        self.ops[q].append(dict(call=("dma_start", (), dict(out=out, in_=in_)), deps=deps, tok=tok, inc=(o, 16)))
        self.nops += 1
        return tok

    def custom(self, eng, fn, r=(), w=(), sem_tile=None, inc=1):
        call = self._rec(fn)
        o = self._own(sem_tile, "cc")
        tok = ["D", o, o.dcnt + inc]
        deps = self._deps(eng, r, w, tok)
        o.dcnt += inc
        self.ops[eng].append(dict(call=call, deps=deps, tok=tok, inc=(o, inc)))
        self.nops += 1
        return tok

    def phase_end(self):
        nc = self.nc
        fin = {}
        for e in ENGS:
            last = None
            for o in self.ops[e]:
                if o["tok"][0] == "E":
                    last = o
            if last is not None:
                last["tok"][3] = True
            c = self.sigbase[e]
            for o in self.ops[e]:
                tok = o["tok"]
                if tok[0] == "E" and tok[3]:
                    c += 1
                    tok[4] = c
            fin[e] = c
        dfin = [(o, o.dcnt) for o in self.owners]
        engobj = {"pe": nc.tensor, "act": nc.scalar, "dve": nc.vector, "pool": nc.gpsimd, "sp": nc.sync}

        with nc.allow_non_contiguous_dma(reason="tiny parameter/edge loads"), nc.Block() as block:
            def body(ename):
                def run(eng):
                    seen_e = self.seen_e[ename]
                    seen_d = self.seen_d[ename]

                    def wait_e(y, cnt):
                        if cnt <= 0 or seen_e.get(y, 0) >= cnt:
                            return
                        seen_e[y] = cnt
                        eng.wait_ge(self.sem[y], cnt)

                    def wait_d(o, cnt):
                        if cnt <= 0 or seen_d.get(id(o), 0) >= cnt:
                            return
                        seen_d[id(o)] = cnt
                        eng.wait_ge(o.dsem, cnt)
                    for op in self.ops[ename]:
                        for d in op["deps"]:
                            if d[0] == "E":
                                wait_e(d[1], d[4])
                            else:
                                wait_d(d[1], d[2])
                        name, a, k = op["call"]
                        ins = getattr(eng, name)(*a, **k)
                        tok = op["tok"]
                        if op["inc"] is not None:
                            ins.then_inc(op["inc"][0].dsem, op["inc"][1])
                        elif tok[3]:
                            ins.then_inc(self.sem[ename], 1)
                    for y in ENGS:
                        if y != ename:
                            wait_e(y, fin[y])
                    for (o, cnt) in dfin:
                        wait_d(o, cnt)
                return run
            block.tensor(body("pe"))
            block.scalar(body("act"))
            block.vector(body("dve"))
            block.gpsimd(body("pool"))
            block.sync(body("sp"))
        self.sigbase = fin
        self.phase_map = {}
        self.cls_next = {"sp": 0, "pool": 0, "cc": 0}
        self.ops = {e: [] for e in ENGS}
        for t in self.alltiles:
            t.lw = []
            t.rd = []


def build(n_layers=NL, dbg=None, NLW=NL, stop_after=None, NCORES=8):
    nc = bass.Bass("TRN2", target_bir_lowering=False)
    dbg = dbg or {}
    MC = 12288 // NCORES
    PAIRS = [[2 * i, 2 * i + 1] for i in range(NCORES // 2)]

    def din(name, shape, dt=F32):
        return nc.dram_tensor(name, list(shape), dt, kind="ExternalInput").ap()

    def dint(name, shape, dt=F32):
        return nc.dram_tensor(name, list(shape), dt, kind="Internal").ap()

    xT_d = din("xT", [128, KC, T])
    cT_d = din("cT", [128, KC, 5])
    sel_d = din("sel", [5, 2])
    selbc_d = din("selbc", [128, 2, 5])
    cst_d = din("cst", [128, 5 * 128])
    selh_d = din("selh", [16, 16 * 128])
    rope_d = din("rope", [128, 2, TLAT])
    msk_d = din("msk", [128, 2])
    w_ada_d = din("w_ada_s", [NLW, D, MC])
    badd_d = din("badd", [5, NLW * MC])
    par_d = din("par", [128, NPAR])
    w_in_d = din("w_in", [NLW, D, 4128]) if not str(stop_after).startswith("0") else None
    w_out_d = din("w_out", [NLW, D, D]) if not str(stop_after).startswith("0") else None
    w_up_d = din("w_up", [NLW, D, 2 * DFF]) if not str(stop_after).startswith("0") else None
    w_dn_d = din("w_down", [NLW, DFF, D]) if not str(stop_after).startswith("0") else None
    yT_d = nc.dram_tensor("yT", [128, KC, TLAT], F32, kind="ExternalOutput").ap()
    dbg_d = {k: nc.dram_tensor("dbg_" + k, list(shp), F32, kind="ExternalOutput").ap() for k, shp in dbg.items()}

    xs_d = dint("xs_d", [128, KC, T])
    z_d = dint("z_d", [128, 8, T], BF16)
    y_d = dint("y_d", [9, 128, 1024])
    loc_mod = dint("loc_mod", [5, NLW * MC])
    gat_mod = dint("gat_mod", [NCORES * 5, NLW * MC])
    loc_e = dint("loc_e", [128, KC * 4], BF16)
    gat_e = dint("gat_e", [256, KC * 4], BF16)
    loc_k = dint("loc_k", [256, T], BF16)
    gat_k = dint("gat_k", [512, T], BF16)
    loc_v = dint("loc_v", [T, 256], BF16)
    gat_v = dint("gat_v", [2 * T, 256], BF16)
    SW = 4 * 1024 + 4 * 16
    loc_sk = [dint(f"loc_sk{k}", [128, 1024]) for k in range(4)]
    gat_sk = [dint(f"gat_sk{k}", [256, 1024]) for k in range(4)]
    loc_sd = dint("loc_sd", [128, 64])
    gat_sd = dint("gat_sd", [256, 64])

    with ExitStack() as st:
        P = Prog(nc, st)
        uid = [0]

        def mk_sb(stack):
            def sb(name, shape, dt=F32):
                uid[0] += 1
                return stack.enter_context(nc.sbuf_tensor(f"{name}_{uid[0]}", list(shape), dt))
            return sb
        sb = mk_sb(st)

        ps = st.enter_context(nc.psum_tensor("ps", [128, 8, 512], F32))
        bank_t = P.tiles(8, "bank")
        bank_i = [0, 0]

        def bank():
            b = bank_i[0]
            bank_i[0] = (b + 1) % 4
            return b, bank_t[b]

        def lbank():
            b = 4 + bank_i[1]
            bank_i[1] = (bank_i[1] + 1) % 4
            return b, bank_t[b]

        xs_own = P.tiles(4, "xsown")
        xs_t = [[P.tile(f"xs{c}_{g}", owner=xs_own[c % 4]) for g in range(3)] for c in range(KC)]
        hT = sb("hT", [128, KC, TH], BF16)
        hT_t = P.tile("hT")
        hTe_t = P.tile("hTe")
        attT = sb("attT", [128, 8, T], BF16)
        att_t = P.tile("attT")
        modT = sb("modT", [128, NL, 96, 2])
        modT_t = P.tile("modT")
        cst = sb("cst", [128, 5 * 128])
        cst_t = P.tile("cst")
        ident_f = cst[:, 0:128]
        ones_f = cst[:, 128:256]
        tri_f32 = [cst[:, 256:384], cst[:, 384:512]]
        pm_f = cst[:, 512:640]
        cstb = sb("cstb", [128, 2 * 128], BF16)
        cstb_t = P.tile("cstb")
        ident_b = cstb[:, 0:128]
        ones_b = cstb[:, 128:256]
        msk = sb("msk", [128, 2])
        msk_t = P.tile("msk")
        mA = msk[:, 0:1]
        mB = msk[:, 1:2]
        par = sb("par", [128, NPAR])

        def pv(name):
            o, shp = PAR_OFF[name]
            n = int(np.prod(shp))
            v = par[:, o:o + n]
            if len(shp) == 2:
                return v.rearrange("p (a b) -> p a b", a=shp[0])
            if len(shp) == 3:
                return v.rearrange("p (a b c) -> p a b c", a=shp[0], b=shp[1])
            return v
        nmw, nfw, qnw, knw = pv("nmw"), pv("nfw"), pv("qnw"), pv("knw")
        scw, scb, snw, fcw, fcb = pv("scw"), pv("scb"), pv("snw"), pv("fcw"), pv("fcb")
        dtb_bc, a_bc, dsk_bc = pv("dtb"), pv("alog"), pv("dsk")
        par_t = P.tile("par")
        scr_sq = [sb(f"scr_sq{i}", [128, 512], BF16) for i in range(2)]
        scr_sq_t = P.tiles(2, "scr_sq")
        scr_f = [sb(f"scr_f{i}", [128, 1024]) for i in range(4)]
        scr_f_t = P.tiles(4, "scr_f")
        rr = {}

        def nxt(key, n):
            v = rr.get(key, 0)
            rr[key] = (v + 1) % n
            return v
        Rn = sb("Rn", [128, T])
        Rn_t = P.tile("Rn")
        asc = sb("asc", [128, KC, 2])
        asc_t = P.tile("asc")
        ge = sb("ge", [128, 2, KC, 4], BF16)
        ge_t = P.tile("ge")
        loce_t = P.tile("loce")
        gate_t = P.tile("gate")
        lock_t, gatk_t, locv_t, gatv_t = P.tile("lock"), P.tile("gatk"), P.tile("locv"), P.tile("gatv")
        locs_t, gats_t = P.tile("locs"), P.tile("gats")
        zd_t = P.tile("zd")
        yd_t = P.tiles(9, "yd", owner=P.tile("ydown"))

        def wview(w2d, col0, ncols):
            return w2d[:, col0:col0 + ncols].rearrange("(kc p) n -> p kc n", p=128)

        with ExitStack() as ph:
            sbp = mk_sb(ph)
            P.dma("sp", cst[:], cst_d, w=[cst_t])
            P.dma("sp", msk[:], msk_d, w=[msk_t])
            P.dma("sp", par[:], par_d, w=[par_t])
            P.op("act", lambda e: e.activation(out=a_bc, in_=a_bc, func=AF.Exp), r=[par_t], w=[par_t])
            P.op("dve", lambda e: e.tensor_scalar(out=a_bc, in0=a_bc, scalar1=-1.0, scalar2=None, op0=ALU.mult), r=[par_t], w=[par_t])
            P.op("dve", lambda e: e.tensor_copy(out=cstb[:], in_=cst[:, 0:256]), r=[cst_t], w=[cstb_t])
            for c in range(KC):
                P.dma("sp", xs_d[:, c, :], xT_d[:, c, :], w=xs_t[c])
            if stop_after == "0a":
                P.phase_end()
                return nc
            wbuf = [sbp(f"wbufm{i}", [128, KC, 512], BF16) for i in range(2)]
            wbuf_t = P.tiles(2, "wbufm")
            cT = sbp("cT", [128, KC, 5])
            cTb = sbp("cTb", [128, KC, 5], BF16)
            cT_t = P.tile("cT")
            sel = sbp("sel", [5, 2])
            sel_t = P.tile("sel")
            modloc = sbp("modloc", [5, NLW * MC])
            modloc_t = P.tile("modloc")
            badd = sbp("badd", [5, NLW * MC])
            badd_t = P.tile("badd")
            P.dma("sp", cT[:], cT_d, w=[cT_t])
            P.dma("sp", sel[:], sel_d, w=[sel_t])
            P.dma("sp", badd[:], badd_d, w=[badd_t])
            P.op("act", lambda e: e.activation(out=cTb[:], in_=cT[:], func=AF.Silu), r=[cT_t], w=[cT_t])
            for l in range(NLW):
                for pc in range(MC // 512):
                    wi = nxt("wm", 2)
                    P.dma("pool", wbuf[wi][:], wview(w_ada_d[l], pc * 512, 512), w=[wbuf_t[wi]])
                    b, bt = bank()
                    for kc in range(KC):
                        P.op("pe", lambda e: e.matmul(ps[0:5, b, :], cTb[:, kc, :], wbuf[wi][:, kc, :], start=(kc == 0), stop=(kc == KC - 1)), r=[cT_t, wbuf_t[wi]], w=[bt])
                    o = l * MC + pc * 512
                    P.op("dve", lambda e: e.tensor_tensor(out=modloc[:, o:o + 512], in0=ps[0:5, b, :], in1=badd[:, o:o + 512], op=ALU.add), r=[bt, badd_t], w=[modloc_t])
            locmod_t = P.tile("locmod")
            gatmod_t = P.tile("gatmod")
            P.dma("sp", loc_mod, modloc[:], r=[modloc_t], w=[locmod_t])
            if stop_after == "0b":
                P.phase_end()
                return nc
            P.custom("pool", lambda e: e.collective_compute("AllGather", ALU.bypass, replica_groups=[list(range(NCORES))], ins=[loc_mod], outs=[gat_mod]), r=[locmod_t], w=[gatmod_t], sem_tile=gatmod_t)
            if stop_after == "0c":
                P.phase_end()
                return nc
            modg = [sbp(f"modg{i}", [5, 1536]) for i in range(2)]
            modg_t = P.tiles(2, "modg")
            selbc = sbp("selbc", [128, 2, 5])
            selbc_t = P.tile("selbc")
            P.dma("sp", selbc[:], selbc_d, w=[selbc_t])
            mtmp = sbp("mtmp", [128, 12, 5])
            mtmp_t = P.tile("mtmp")
            NQ = MC // 1536
            for l in range(NLW):
                for j in range(NCORES):
                    for q in range(NQ):
                        mi = nxt("mg", 2)
                        P.dma("sp", modg[mi][:], gat_mod[j * 5:(j + 1) * 5, l * MC + q * 1536:l * MC + (q + 1) * 1536], r=[gatmod_t], w=[modg_t[mi]])
                        b, bt = bank()
                        for cc in range(12):
                            P.op("pe", lambda e: e.transpose(out=ps[:, b, cc * 5:cc * 5 + 5], in_=modg[mi][:, cc * 128:(cc + 1) * 128], identity=ident_f[0:5, 0:5]), r=[modg_t[mi], cst_t], w=[bt])
                        c0 = (j * NQ + q) * 12
                        for r_ in range(2):
                            P.op("dve", lambda e: e.tensor_tensor(out=mtmp[:], in0=ps[:, b, 0:60].rearrange("p (c k) -> p c k", k=5), in1=selbc[:, r_:r_ + 1, :].to_broadcast([128, 12, 5]), op=ALU.mult), r=[bt, selbc_t, mtmp_t], w=[mtmp_t])
                            P.op("dve", lambda e: e.tensor_reduce(out=modT[:, l, c0:c0 + 12, r_], in_=mtmp[:], axis=mybir.AxisListType.X, op=ALU.add), r=[mtmp_t], w=[modT_t])
            if "modT" in dbg:
                P.dma("sp", dbg_d["modT"][:, 0:NLW], modT[:, 0:NLW], r=[modT_t], w=[])
            P.phase_end()
        if stop_after == "0":
            return nc

        def mod(l, m, c, r):
            return modT[:, l, m * 16 + c, r:r + 1]

        def rms_modulate(l, wtile, m_sc, m_sh):
            for gi, (o, n) in enumerate(TT):
                b, bt = bank()
                for c in range(KC):
                    f = nxt("f", 4)
                    P.dma("sp", scr_f[f][:, 0:n], xs_d[:, c, o:o + n], r=[xs_t[c][gi]], w=[scr_f_t[f]])
                    i = nxt("sq", 2)
                    P.op("act", lambda e: e.activation(out=scr_sq[i][:, 0:n], in_=scr_f[f][:, 0:n], func=AF.Square), r=[scr_f_t[f]], w=[scr_sq_t[i]])
                    P.op("pe", lambda e: e.matmul(ps[:, b, 0:n], ones_b, scr_sq[i][:, 0:n], start=(c == 0), stop=(c == KC - 1)), r=[scr_sq_t[i], cstb_t], w=[bt])
                P.op("act", lambda e: e.activation(out=Rn[:, o:o + n], in_=ps[:, b, 0:n], func=AF.Sqrt, scale=1.0 / D, bias=EPS), r=[bt], w=[Rn_t])
            P.op("dve", lambda e: e.reciprocal(out=Rn[:], in_=Rn[:]), r=[Rn_t], w=[Rn_t])
            P.op("dve", lambda e: e.scalar_tensor_tensor(out=asc[:], in0=modT[:, l, m_sc * 16:(m_sc + 1) * 16, :], scalar=1.0, in1=wtile[:, l, :].unsqueeze(2).to_broadcast([128, KC, 2]), op0=ALU.add, op1=ALU.mult), r=[modT_t, par_t], w=[asc_t])
            for c in range(KC):
                for gi, (o, n) in enumerate(TT):
                    r_ = 1 if gi == 0 else 0
                    f = nxt("f", 4)
                    P.dma("sp", scr_f[f][:, 0:n], xs_d[:, c, o:o + n], r=[xs_t[c][gi]], w=[scr_f_t[f]])
                    P.op("dve", lambda e: e.tensor_tensor(out=scr_f[f][:, 0:n], in0=scr_f[f][:, 0:n], in1=Rn[:, o:o + n], op=ALU.mult), r=[Rn_t, scr_f_t[f]], w=[scr_f_t[f]])
                    P.op("act", lambda e: e.activation(out=hT[:, c, o:o + n], in_=scr_f[f][:, 0:n], func=AF.Identity, scale=asc[:, c, r_:r_ + 1], bias=mod(l, m_sh, c, r_)), r=[scr_f_t[f], asc_t, modT_t], w=[hT_t])

        ecols = [0, 127, 128, 1151]

        def edge_exchange():
            for e_i, col in enumerate(ecols):
                P.op("pool", lambda e: e.tensor_copy(out=ge[:, 0, :, e_i], in_=hT[:, :, col]), r=[hT_t], w=[ge_t])
            P.dma("sp", loc_e, ge[:, 0, :, :].rearrange("p k e -> p (k e)"), r=[ge_t], w=[loce_t])
            P.custom("pool", lambda e: e.collective_compute("AllGather", ALU.bypass, replica_groups=PAIRS, ins=[loc_e], outs=[gat_e]), r=[loce_t], w=[gate_t], sem_tile=gate_t)
            P.dma("sp", ge[:], gat_e.rearrange("(r p) (k e) -> p r k e", r=2, e=4), r=[gate_t], w=[ge_t])
            for dst, rnk, e_i, m in [(1152, 0, 1, mB), (1153, 1, 0, mA), (1154, 0, 3, mB), (1155, 1, 2, mA)]:
                P.op("dve", lambda e: e.tensor_scalar(out=hT[:, :, dst], in0=ge[:, rnk, :, e_i], scalar1=m, scalar2=None, op0=ALU.mult), r=[ge_t, msk_t], w=[hTe_t])

        def proj_fm_conv(env, wb, wt, oc, need_ctx):
            ui = nxt("u", len(env["upre"]))
            u = env["upre"][ui]
            ut = env["upre_t"][ui]
            tts = TT if need_ctx else TT[1:]
            dsts = {0: 1, 128: 131, 640: 643}
            bh, bht = bank()
            for kc in range(KC):
                P.op("pe", lambda e: e.matmul(ps[:, bh, 0:4], wb[:, kc, oc * 128:(oc + 1) * 128], hT[:, kc, T:TH], start=(kc == 0), stop=(kc == KC - 1)), r=[wt, hT_t, hTe_t], w=[bht])
            for src, dcol in [(0, 0), (1, 129), (2, 130), (3, 1155)]:
                P.op("act", lambda e: e.activation(out=u[:, dcol:dcol + 1], in_=ps[:, bh, src:src + 1], func=AF.Copy), r=[bht], w=[ut])
            for (o, n) in tts:
                b, bt = bank()
                for kc in range(KC):
                    P.op("pe", lambda e: e.matmul(ps[:, b, 0:n], wb[:, kc, oc * 128:(oc + 1) * 128], hT[:, kc, o:o + n], start=(kc == 0), stop=(kc == KC - 1)), r=[wt, hT_t], w=[bt])
                P.op("act", lambda e: e.activation(out=u[:, dsts[o]:dsts[o] + n], in_=ps[:, b, 0:n], func=AF.Copy), r=[bt], w=[ut])
            return ui

        def conv3(env, ui, w3, need_ctx):
            ci = nxt("m", len(env["cacc"]))
            u = env["upre"][ui]
            ut = env["upre_t"][ui]
            a = env["cacc"][ci]
            at = env["cacc_t"][ci]
            grp = [(0, 0, 128), (130, 128, 1024)] if need_ctx else [(130, 128, 1024)]
            for (io, oo, n) in grp:
                P.op("dve", lambda e: e.tensor_scalar(out=a[:, oo:oo + n], in0=u[:, io:io + n], scalar1=w3[0], scalar2=None, op0=ALU.mult), r=[ut, par_t], w=[at])
                P.op("dve", lambda e: e.scalar_tensor_tensor(out=a[:, oo:oo + n], in0=u[:, io + 1:io + 1 + n], scalar=w3[1], in1=a[:, oo:oo + n], op0=ALU.mult, op1=ALU.add), r=[ut, par_t, at], w=[at])
                P.op("dve", lambda e: e.scalar_tensor_tensor(out=a[:, oo:oo + n], in0=u[:, io + 2:io + 2 + n], scalar=w3[2], in1=a[:, oo:oo + n], op0=ALU.mult, op1=ALU.add), r=[ut, par_t, at], w=[at])
            return ci

        def mm16(b, bt, wb, wt, oc, o, n, rhs, rhs_t):
            for kc in range(KC):
                P.op("pe", lambda e: e.matmul(ps[:, b, 0:n], wb[:, kc, oc * 128:(oc + 1) * 128], rhs[:, kc, o:o + n], start=(kc == 0), stop=(kc == KC - 1)), r=[wt] + rhs_t, w=[bt])

        def dump(key, src3, tiles, nchunk, ncol):
            for c in range(nchunk):
                o = 0
                while o < ncol:
                    n = min(1024, ncol - o)
                    f = nxt("f", 4)
                    P.op("dve", lambda e: e.tensor_copy(out=scr_f[f][:, 0:n], in_=src3[:, c, o:o + n]), r=tiles, w=[scr_f_t[f]])
                    P.dma("sp", dbg_d[key][:, c, o:o + n], scr_f[f][:, 0:n], r=[scr_f_t[f]], w=[])
                    o += n

        for l in range(n_layers):
            need_ctx = l < NL - 1
            w_in_l = w_in_d[l]
            with ExitStack() as ph:
                sbp = mk_sb(ph)
                rope = sbp("rope", [128, 2, TLAT])
                rope_t = P.tile("rope")
                P.dma("sp", rope[:], rope_d, w=[rope_t])
                wbuf = [sbp(f"wbufa{i}", [128, KC, 256], BF16) for i in range(2)]
                wbuf_t = P.tiles(2, "wbufa")
                qT = sbp("qT", [128, 8, T], BF16)
                qT_t = P.tiles(8, "qT")
                kTl = sbp("kTl", [128, 2, T], BF16)
                kTl_t = P.tile("kTl")
                vl = sbp("vl", [128, 9, 256], BF16)
                vl_t = P.tile("vl")
                kTg = sbp("kTg", [128, 2, 2, T], BF16)
                kTg_t = P.tile("kTg")
                vg = sbp("vg", [128, 18, 256], BF16)
                vg_t = P.tile("vg")
                pT = [sbp(f"pT{i}", [128, 512], BF16) for i in range(3)]
                pT_t = P.tiles(3, "pT")

                def qk_norm_rope(b, bt, gi, dest, dest_t, wv):
                    o, n = TT[gi]
                    i = nxt("sq", 2)
                    P.op("act", lambda e: e.activation(out=scr_sq[i][:, 0:n], in_=ps[:, b, 0:n], func=AF.Square), r=[bt], w=[scr_sq_t[i]])
                    b2, bt2 = bank()
                    P.op("pe", lambda e: e.matmul(ps[:, b2, 0:n], ones_b, scr_sq[i][:, 0:n], start=True, stop=True), r=[scr_sq_t[i], cstb_t], w=[bt2])
                    f1 = nxt("f", 4)
                    P.op("act", lambda e: e.activation(out=scr_f[f1][:, 0:n], in_=ps[:, b2, 0:n], func=AF.Sqrt, scale=1.0 / 128, bias=EPS), r=[bt2], w=[scr_f_t[f1]])
                    P.op("dve", lambda e: e.reciprocal(out=scr_f[f1][:, 0:n], in_=scr_f[f1][:, 0:n]), r=[scr_f_t[f1]], w=[scr_f_t[f1]])
                    if gi == 0:
                        P.op("dve", lambda e: e.scalar_tensor_tensor(out=dest[:, o:o + n], in0=ps[:, b, 0:n], scalar=wv[:, l:l + 1], in1=scr_f[f1][:, 0:n], op0=ALU.mult, op1=ALU.mult), r=[bt, scr_f_t[f1], par_t], w=[dest_t])
                        return
                    f2 = nxt("f", 4)
                    P.op("dve", lambda e: e.scalar_tensor_tensor(out=scr_f[f2][:, 0:n], in0=ps[:, b, 0:n], scalar=wv[:, l:l + 1], in1=scr_f[f1][:, 0:n], op0=ALU.mult, op1=ALU.mult), r=[bt, scr_f_t[f1], par_t], w=[scr_f_t[f2]])
                    b3, bt3 = bank()
                    P.op("pe", lambda e: e.matmul(ps[:, b3, 0:n], pm_f, scr_f[f2][:, 0:n], start=True, stop=True), r=[scr_f_t[f2], cst_t], w=[bt3])
                    lo = o - 128
                    P.op("pool", lambda e: e.tensor_tensor(out=scr_f[f2][:, 0:n], in0=scr_f[f2][:, 0:n], in1=rope[:, 0, lo:lo + n], op=ALU.mult), r=[scr_f_t[f2], rope_t, bt3], w=[scr_f_t[f2]])
                    P.op("dve", lambda e: e.tensor_tensor(out=scr_f[f1][:, 0:n], in0=ps[:, b3, 0:n], in1=rope[:, 1, lo:lo + n], op=ALU.mult), r=[bt3, rope_t], w=[scr_f_t[f1]])
                    P.op("pool", lambda e: e.tensor_tensor(out=dest[:, o:o + n], in0=scr_f[f1][:, 0:n], in1=scr_f[f2][:, 0:n], op=ALU.add), r=[scr_f_t[f1], scr_f_t[f2]], w=[dest_t])

                rms_modulate(l, nmw, 1, 0)
                edge_exchange()
                if "h" in dbg and l == 0:
                    dump("h", hT, [hT_t, hTe_t], KC, TH)
                for blk in range(6):
                    wi = nxt("wa", 2)
                    wb, wt = wbuf[wi], wbuf_t[wi]
                    P.dma("pool", wb[:], wview(w_in_l, blk * 256, 256), w=[wt])
                    if blk < 4:
                        for oc in range(2):
                            h = blk * 2 + oc
                            for gi, (o, n) in enumerate(TT):
                                b, bt = bank()
                                mm16(b, bt, wb, wt, oc, o, n, hT, [hT_t])
                                qk_norm_rope(b, bt, gi, qT[:, h, :], qT_t[h], qnw)
                    elif blk == 4:
                        for oc in range(2):
                            for gi, (o, n) in enumerate(TT):
                                b, bt = bank()
                                mm16(b, bt, wb, wt, oc, o, n, hT, [hT_t])
                                qk_norm_rope(b, bt, gi, kTl[:, oc, :], kTl_t, knw)
                    else:
                        for i in range(9):
                            b, bt = bank()
                            for kc in range(KC):
                                P.op("pe", lambda e: e.matmul(ps[:, b, 0:256], hT[:, kc, i * 128:(i + 1) * 128], wb[:, kc, :], start=(kc == 0), stop=(kc == KC - 1)), r=[wt, hT_t], w=[bt])
                            P.op("act", lambda e: e.activation(out=vl[:, i, :], in_=ps[:, b, 0:256], func=AF.Copy), r=[bt], w=[vl_t])
                P.dma("sp", loc_k.rearrange("(g d) t -> d g t", g=2), kTl[:], r=[kTl_t], w=[lock_t])
                P.dma("sp", loc_v.rearrange("(i p) c -> p i c", p=128), vl[:], r=[vl_t], w=[locv_t])
                P.custom("pool", lambda e: e.collective_compute("AllGather", ALU.bypass, replica_groups=PAIRS, ins=[loc_k], outs=[gat_k]), r=[lock_t], w=[gatk_t], sem_tile=gatk_t)
                P.custom("pool", lambda e: e.collective_compute("AllGather", ALU.bypass, replica_groups=PAIRS, ins=[loc_v], outs=[gat_v]), r=[locv_t], w=[gatv_t], sem_tile=gatv_t)
                for g_ in range(2):
                    for r_ in range(2):
                        P.dma("sp", kTg[:, g_, r_, :], gat_k[(r_ * 2 + g_) * 128:(r_ * 2 + g_ + 1) * 128, :], r=[gatk_t], w=[kTg_t])
                P.dma("sp", vg[:], gat_v.rearrange("(kt p) c -> p kt c", p=128), r=[gatv_t], w=[vg_t])
                if "kg" in dbg and l == 0:
                    dump("kg", kTg[:].rearrange("p g r t -> p (g r) t"), [kTg_t], 4, T)
                    dump("vg", vg, [vg_t], 18, 256)
                if "q" in dbg and l == 0:
                    dump("q", qT, qT_t, 8, T)
                    dump("k", kTl, [kTl_t], 2, T)
                scale = 128 ** -0.5
                qtiles = [(128, 512, list(range(18))), (640, 512, list(range(18)))]
                if need_ctx:
                    qtiles.append((0, 128, [0, 9]))
                for g in range(2):
                    for j in range(4):
                        h = g * 4 + j
                        for (qo, qn, kts) in qtiles:
                            bo, bot = lbank()
                            bl, blt = lbank()
                            for ki, kt in enumerate(kts):
                                r_, i_ = kt // 9, kt % 9
                                bs_, bst = bank()
                                P.op("pe", lambda e: e.matmul(ps[:, bs_, 0:qn], kTg[:, g, r_, i_ * 128:(i_ + 1) * 128], qT[:, h, qo:qo + qn], start=True, stop=True), r=[kTg_t, qT_t[h]], w=[bst])
                                pi = nxt("p", 3)
                                if "S0" in dbg and l == 0 and h == 0 and qo == 128 and ki == 0:
                                    fdb = nxt("f", 4)
                                    P.op("dve", lambda e: e.tensor_copy(out=scr_f[fdb][:, 0:qn], in_=ps[:, bs_, 0:qn]), r=[bst], w=[scr_f_t[fdb]])
                                    P.dma("sp", dbg_d["S0"][:, 0, :], scr_f[fdb][:, 0:qn], r=[scr_f_t[fdb]], w=[])
                                P.op("act", lambda e: e.activation(out=pT[pi][:, 0:qn], in_=ps[:, bs_, 0:qn], func=AF.Exp, scale=scale), r=[bst], w=[pT_t[pi]])
                                if "S0" in dbg and l == 0 and h == 0 and qo == 128 and ki == 0:
                                    dump("P0", pT[pi][:].rearrange("p (o n) -> p o n", o=1), [pT_t[pi]], 1, 512)
                                P.op("pe", lambda e: e.matmul(ps[:, bo, 0:qn], vg[:, kt, g * 128:(g + 1) * 128], pT[pi][:, 0:qn], start=(ki == 0), stop=(ki == len(kts) - 1)), r=[vg_t, pT_t[pi]], w=[bot])
                                P.op("pe", lambda e: e.matmul(ps[:, bl, 0:qn], ones_b, pT[pi][:, 0:qn], start=(ki == 0), stop=(ki == len(kts) - 1)), r=[cstb_t, pT_t[pi]], w=[blt])
                            if "S0" in dbg and l == 0 and h == 0 and qo == 128:
                                for key_, bk_, bkt_ in (("L0", bl, blt), ("O0", bo, bot)):
                                    fdb = nxt("f", 4)
                                    P.op("dve", lambda e: e.tensor_copy(out=scr_f[fdb][:, 0:qn], in_=ps[:, bk_, 0:qn]), r=[bkt_], w=[scr_f_t[fdb]])
                                    P.dma("sp", dbg_d[key_][:, 0, :], scr_f[fdb][:, 0:qn], r=[scr_f_t[fdb]], w=[])
                            f1 = nxt("f", 4)
                            P.op("dve", lambda e: e.reciprocal(out=scr_f[f1][:, 0:qn], in_=ps[:, bl, 0:qn]), r=[blt], w=[scr_f_t[f1]])
                            P.op("dve", lambda e: e.tensor_tensor(out=attT[:, h, qo:qo + qn], in0=ps[:, bo, 0:qn], in1=scr_f[f1][:, 0:qn], op=ALU.mult), r=[bot, scr_f_t[f1]], w=[att_t])
                if "att" in dbg and l == 0:
                    dump("att", attT, [att_t], 8, T)
                P.phase_end()
            if stop_after == "A":
                return nc

            with ExitStack() as ph:
                sbp = mk_sb(ph)
                env = dict(upre=[sbp("upre", [128, TH])], upre_t=P.tiles(1, "upre"), cacc=[sbp("cacc", [128, T])], cacc_t=P.tiles(1, "cacc"))
                wbuf = [sbp(f"wbufb{i}", [128, KC, 256], BF16) for i in range(2)]
                wbuf_t = P.tiles(2, "wbufb")
                selh = sbp("selh", [16, 16 * 128])
                selh_t = P.tile("selh")
                P.dma("sp", selh[:], selh_d, w=[selh_t])
                bcT = sbp("bcT", [128, 4, T], BF16)
                bcT_t = P.tiles(4, "bcT")
                xcT = [sbp(f"xcT{i}", [128, T], BF16) for i in range(2)]
                xcT_t = P.tiles(2, "xcT")
                zst = [sbp(f"zst{i}", [128, T], BF16) for i in range(2)]
                zst_t = P.tiles(2, "zst")
                xtok = sbp("xtok", [128, 9, 1024], BF16)
                xtok_t = P.tiles(9, "xtok")
                btok = sbp("btok", [128, 9, 256], BF16)
                btok_t = P.tiles(9, "btok")
                wdt = sbp("wdt", [128, KC, 32], BF16)
                wdt_t = P.tile("wdt")
                dt_tok = sbp("dt_tok", [128, 9, 32])
                da_tok = sbp("da_tok", [128, 9, 32])
                dtk_t = P.tile("dtk")
                stt = [sbp(f"stt{d}", [128, 1024]) for d in range(2)]
                stt_t = P.tiles(2, "stt")
                prevb = [sbp(f"prevb{d}", [128, 1024], BF16) for d in range(2)]
                prevb_t = P.tiles(2, "prevb")
                dacc = sbp("dacc", [128, 16])
                dacc_t = P.tile("dacc")
                dme = sbp("dme", [128, 4, 16])
                dg0 = sbp("dg0", [128, 4, 16])
                dg1 = sbp("dg1", [128, 4, 16])
                dsm_t = P.tile("dsm")
                cs_tok = sbp("cs_tok", [128, 2, 16])
                cs_fm = sbp("cs_fm", [16, 128])
                cd = sbp("cd", [128, 16])
                wtk = sbp("wtk", [128, 16])
                pre_t = P.tile("pre")
                xsc = sbp("xsc", [128, 1024], BF16)
                xsc_t = P.tile("xsc")
                cbm = sbp("cbm", [128, 2, 128])
                cbm_t = P.tile("cbm")
                hd = [dict(diff=sbp(f"hdiff{i}", [128, 128]), E=sbp(f"hE{i}", [128, 128]), MT=sbp(f"hMT{i}", [128, 128], BF16),
                           EB=sbp(f"hEB{i}", [128, 128]), Cs=sbp(f"hCs{i}", [128, 128], BF16)) for i in range(2)]
                hd_t = [{k: P.tile(f"hd{i}{k}") for k in ["diff", "E", "MT", "EB", "Cs"]} for i in range(2)]
                dexp = sbp("dexp", [128, 1024])
                dexp_t = P.tile("dexp")
                ztl = sbp("ztl", [128, 8, 128], BF16)
                ztl_t = P.tile("ztl")
                szt = sbp("szt", [128, 8, 128])
                szt_t = P.tile("szt")
                gT = sbp("gT", [128, 8, 128])
                gT_t = P.tile("gT")
                gsq = sbp("gsq", [128, 8, 128], BF16)
                gsq_t = P.tile("gsq")
                grs = sbp("grs", [128, 128])
                grs_t = P.tile("grs")

                for blk in range(6, 16):
                    wi = nxt("wb", 2)
                    wb, wt = wbuf[wi], wbuf_t[wi]
                    P.dma("pool", wb[:], wview(w_in_l, blk * 256, 256), w=[wt])
                    for oc in range(2):
                        if blk < 10:
                            zc = (blk - 6) * 2 + oc
                            zi = nxt("z", 2)
                            for gi, (o, n) in enumerate(TT):
                                b, bt = bank()
                                mm16(b, bt, wb, wt, oc, o, n, hT, [hT_t])
                                P.op("act", lambda e: e.activation(out=zst[zi][:, o:o + n], in_=ps[:, b, 0:n], func=AF.Copy), r=[bt], w=[zst_t[zi]])
                            P.dma("sp", z_d[:, zc, :], zst[zi][:], r=[zst_t[zi]], w=[zd_t])
                        else:
                            xc = (blk - 10) * 2 + oc
                            ui = proj_fm_conv(env, wb, wt, oc, True)
                            ci = conv3(env, ui, [scw[:, l, k, xc:xc + 1] for k in range(3)], True)
                            if xc < 8:
                                xi = nxt("x", 2)
                                P.op("act", lambda e: e.activation(out=xcT[xi][:], in_=env["cacc"][ci][:], func=AF.Silu, bias=scb[:, l, xc:xc + 1]), r=[env["cacc_t"][ci], par_t], w=[xcT_t[xi]])
                                for i0 in range(0, 9, 4):
                                    ii = list(range(i0, min(9, i0 + 4)))
                                    b, bt = bank()
                                    pb = ps[:, b, :].bitcast(BF16)
                                    for k_, i in enumerate(ii):
                                        P.op("pe", lambda e: e.transpose(out=pb[:, k_ * 128:(k_ + 1) * 128], in_=xcT[xi][:, i * 128:(i + 1) * 128], identity=ident_b), r=[xcT_t[xi], cstb_t], w=[bt])
                                    P.op("act", lambda e: e.activation(out=xtok[:, i0:i0 + len(ii), xc * 128:(xc + 1) * 128], in_=pb[:, 0:len(ii) * 128].rearrange("p (i c) -> p i c", c=128), func=AF.Copy), r=[bt], w=xtok_t[i0:i0 + len(ii)])
                            else:
                                bc = xc - 8
                                P.op("act", lambda e: e.activation(out=bcT[:, bc, :], in_=env["cacc"][ci][:], func=AF.Silu, bias=scb[:, l, xc:xc + 1]), r=[env["cacc_t"][ci], par_t], w=[bcT_t[bc]])
                                if bc < 2:
                                    for i0 in range(0, 9, 4):
                                        ii = list(range(i0, min(9, i0 + 4)))
                                        b, bt = bank()
                                        pb = ps[:, b, :].bitcast(BF16)
                                        for k_, i in enumerate(ii):
                                            P.op("pe", lambda e: e.transpose(out=pb[:, k_ * 128:(k_ + 1) * 128], in_=bcT[:, bc, i * 128:(i + 1) * 128], identity=ident_b), r=[bcT_t[bc], cstb_t], w=[bt])
                                        P.op("act", lambda e: e.activation(out=btok[:, i0:i0 + len(ii), bc * 128:(bc + 1) * 128], in_=pb[:, 0:len(ii) * 128].rearrange("p (i c) -> p i c", c=128), func=AF.Copy), r=[bt], w=btok_t[i0:i0 + len(ii)])
                P.dma("pool", wdt[:], wview(w_in_l, 4096, 32), w=[wdt_t])
                for i in range(9):
                    b, bt = bank()
                    for kc in range(KC):
                        P.op("pe", lambda e: e.matmul(ps[:, b, 0:32], hT[:, kc, i * 128:(i + 1) * 128], wdt[:, kc, :], start=(kc == 0), stop=(kc == KC - 1)), r=[wdt_t, hT_t], w=[bt])
                    P.op("dve", lambda e: e.tensor_tensor(out=dt_tok[:, i, :], in0=ps[:, b, 0:32], in1=dtb_bc[:, l, :], op=ALU.add), r=[bt, par_t], w=[dtk_t])
                P.op("act", lambda e: e.activation(out=dt_tok[:], in_=dt_tok[:], func=AF.Exp), r=[dtk_t], w=[dtk_t])
                P.op("act", lambda e: e.activation(out=dt_tok[:], in_=dt_tok[:], func=AF.Ln, bias=1.0), r=[dtk_t], w=[dtk_t])
                P.op("dve", lambda e: e.tensor_tensor(out=da_tok[:], in0=dt_tok[:], in1=a_bc[:, l, :].unsqueeze(1).to_broadcast([128, 9, 32]), op=ALU.mult), r=[dtk_t, par_t], w=[dtk_t])
                P.op("pool", lambda e: e.tensor_copy(out=dexp[:].rearrange("p (h q) -> p h q", h=16), in_=dsk_bc[:, l, :].unsqueeze(2).to_broadcast([128, 16, 64])), r=[par_t], w=[dexp_t])
                if "xtok" in dbg and l == 0:
                    dump("xtok", xtok, xtok_t, 9, 1024)
                    dump("bc", bcT, bcT_t, 4, T)
                    dump("dt", dt_tok, [dtk_t], 9, 32)

                if stop_after == "B1":
                    P.phase_end()
                    return nc

                def chunk_pre(i, d):
                    da = da_tok[:, i, d * 16:(d + 1) * 16]
                    b, bt = bank()
                    P.op("pe", lambda e: e.matmul(ps[:, b, 0:16], tri_f32[d], da, start=True, stop=True), r=[dtk_t, cst_t], w=[bt])
                    P.op("pe", lambda e: e.matmul(ps[:, b, 16:32], ones_f, da, start=True, stop=True), r=[dtk_t, cst_t], w=[bt])
                    P.op("pe", lambda e: e.matmul(ps[0:16, b, 128:256], da, tri_f32[d], start=True, stop=True), r=[dtk_t, cst_t], w=[bt])
                    P.op("dve", lambda e: e.tensor_copy(out=cs_tok[:, 0, :], in_=ps[:, b, 0:16]), r=[bt], w=[pre_t])
                    P.op("dve", lambda e: e.tensor_copy(out=cs_fm[:], in_=ps[0:16, b, 128:256]), r=[bt], w=[pre_t])
                    P.op("act", lambda e: e.activation(out=cd[:], in_=ps[:, b, 16:32], func=AF.Exp), r=[bt], w=[pre_t])
                    P.op("dve", lambda e: e.tensor_tensor(out=cs_tok[:, 1, :], in0=ps[:, b, 16:32], in1=cs_tok[:, 0, :], op=ALU.subtract), r=[bt, pre_t], w=[pre_t])
                    P.op("act", lambda e: e.activation(out=wtk[:], in_=cs_tok[:, 1, :], func=AF.Exp), r=[pre_t], w=[pre_t])
                    P.op("dve", lambda e: e.tensor_tensor(out=wtk[:], in0=wtk[:], in1=dt_tok[:, i, d * 16:(d + 1) * 16], op=ALU.mult), r=[pre_t, dtk_t], w=[pre_t])
                    P.op("dve", lambda e: e.tensor_tensor(out=xsc[:].rearrange("p (h q) -> p h q", h=16), in0=xtok[:, i, :].rearrange("p (h q) -> p h q", h=16), in1=wtk[:].unsqueeze(2).to_broadcast([128, 16, 64]), op=ALU.mult), r=[pre_t, xtok_t[i]], w=[xsc_t])
                    bs = []
                    for g in range(2):
                        b2, bt2 = lbank()
                        P.op("pe", lambda e: e.matmul(ps[:, b2, :], btok[:, i, g * 128:(g + 1) * 128], xsc[:, g * 512:(g + 1) * 512], start=True, stop=True), r=[btok_t[i], xsc_t], w=[bt2])
                        bs.append((b2, bt2))
                    return bs

                def v8(ap):
                    return ap.rearrange("p (h q) -> p h q", q=64)

                def state_update(d, bs):
                    for g in range(2):
                        b2, bt2 = bs[g]
                        sl = stt[d][:, g * 512:(g + 1) * 512]
                        P.op("dve", lambda e: e.tensor_tensor(out=v8(sl), in0=v8(sl), in1=cd[:, g * 8:(g + 1) * 8].unsqueeze(2).to_broadcast([128, 8, 64]), op=ALU.mult), r=[pre_t, stt_t[d]], w=[stt_t[d]])
                        P.op("dve", lambda e: e.tensor_tensor(out=sl, in0=sl, in1=ps[:, b2, :], op=ALU.add), r=[bt2, stt_t[d]], w=[stt_t[d]])

                def bc16(ap16):
                    return ap16.unsqueeze(2).to_broadcast([128, 16, 64])

                segs = [[0], list(range(1, 9))]
                for si, seg in enumerate(segs):
                    for d in range(2):
                        k = si * 2 + d
                        order = seg if d == 0 else seg[::-1]
                        P.op("pool", lambda e: e.memset(stt[d][:], 0.0), r=[], w=[stt_t[d]])
                        P.op("pool", lambda e: e.memset(dacc[:], 1.0), r=[], w=[dacc_t])
                        for i in order:
                            bs = chunk_pre(i, d)
                            state_update(d, bs)
                            P.op("dve", lambda e: e.tensor_tensor(out=dacc[:], in0=dacc[:], in1=cd[:], op=ALU.mult), r=[pre_t, dacc_t], w=[dacc_t])
                        P.dma("sp", loc_sk[k], stt[d][:], r=[stt_t[d]], w=[locs_t])
                        P.dma("sp", loc_sd[:, k * 16:(k + 1) * 16], dacc[:], r=[dacc_t], w=[locs_t])
                if stop_after == "B2":
                    P.phase_end()
                    return nc
                for li_, go_ in list(zip(loc_sk, gat_sk)) + [(loc_sd, gat_sd)]:
                    P.custom("pool", lambda e: e.collective_compute("AllGather", ALU.bypass, replica_groups=PAIRS, ins=[li_], outs=[go_]), r=[locs_t], w=[gats_t], sem_tile=gats_t)
                P.dma("sp", dme[:].rearrange("p k n -> p (k n)"), loc_sd, r=[locs_t], w=[dsm_t])
                P.dma("sp", dg0[:].rearrange("p k n -> p (k n)"), gat_sd[0:128, :], r=[gats_t], w=[dsm_t])
                P.dma("sp", dg1[:].rearrange("p k n -> p (k n)"), gat_sd[128:256, :], r=[gats_t], w=[dsm_t])
                P.op("dve", lambda e: e.tensor_scalar(out=dg0[:], in0=dg0[:], scalar1=mB, scalar2=None, op0=ALU.mult), r=[dsm_t, msk_t], w=[dsm_t])
                P.op("dve", lambda e: e.scalar_tensor_tensor(out=dg0[:], in0=dg1[:], scalar=mA, in1=dg0[:], op0=ALU.mult, op1=ALU.add), r=[dsm_t, msk_t], w=[dsm_t])

                if stop_after == "B3":
                    P.phase_end()
                    return nc

                def load_other(k, f):
                    f2 = nxt("f", 4)
                    assert f2 != f
                    P.dma("sp", scr_f[f][:], gat_sk[k][0:128, :], r=[gats_t], w=[scr_f_t[f]])
                    P.dma("sp", scr_f[f2][:], gat_sk[k][128:256, :], r=[gats_t], w=[scr_f_t[f2]])
                    P.op("dve", lambda e: e.tensor_scalar(out=scr_f[f][:], in0=scr_f[f][:], scalar1=mB, scalar2=None, op0=ALU.mult), r=[scr_f_t[f], msk_t], w=[scr_f_t[f]])
                    P.op("dve", lambda e: e.scalar_tensor_tensor(out=scr_f[f][:], in0=scr_f[f2][:], scalar=mA, in1=scr_f[f][:], op0=ALU.mult, op1=ALU.add), r=[scr_f_t[f], scr_f_t[f2], msk_t], w=[scr_f_t[f]])

                def v16(ap):
                    return ap.rearrange("p (h q) -> p h q", h=16)

                def incoming_state(si, d):
                    m1, m2 = (mA, mB) if d == 0 else (mB, mA)
                    kc_, kl_ = d, 2 + d
                    fo = nxt("f", 4)
                    load_other(kc_, fo)
                    if si == 0:
                        P.op("dve", lambda e: e.tensor_scalar(out=stt[d][:], in0=scr_f[fo][:], scalar1=m2, scalar2=None, op0=ALU.mult), r=[scr_f_t[fo], msk_t], w=[stt_t[d]])
                        return
                    fm = nxt("f", 4)
                    P.dma("sp", scr_f[fm][:], loc_sk[kc_], r=[locs_t], w=[scr_f_t[fm]])
                    P.op("dve", lambda e: e.tensor_tensor(out=v16(stt[d][:]), in0=v16(scr_f[fm][:]), in1=bc16(dg0[:, kc_, :]), op=ALU.mult), r=[scr_f_t[fm], dsm_t], w=[stt_t[d]])
                    P.op("dve", lambda e: e.tensor_tensor(out=stt[d][:], in0=stt[d][:], in1=scr_f[fo][:], op=ALU.add), r=[scr_f_t[fo], stt_t[d]], w=[stt_t[d]])
                    P.op("dve", lambda e: e.tensor_scalar(out=stt[d][:], in0=stt[d][:], scalar1=m1, scalar2=None, op0=ALU.mult), r=[stt_t[d], msk_t], w=[stt_t[d]])
                    P.op("dve", lambda e: e.tensor_tensor(out=v16(scr_f[fo][:]), in0=v16(scr_f[fo][:]), in1=bc16(dme[:, kc_, :]), op=ALU.mult), r=[scr_f_t[fo], dsm_t], w=[scr_f_t[fo]])
                    P.op("dve", lambda e: e.tensor_tensor(out=scr_f[fo][:], in0=scr_f[fo][:], in1=scr_f[fm][:], op=ALU.add), r=[scr_f_t[fo], scr_f_t[fm]], w=[scr_f_t[fo]])
                    P.op("dve", lambda e: e.tensor_tensor(out=v16(scr_f[fo][:]), in0=v16(scr_f[fo][:]), in1=bc16(dg0[:, kl_, :]), op=ALU.mult), r=[scr_f_t[fo], dsm_t], w=[scr_f_t[fo]])
                    load_other(kl_, fm)
                    P.op("dve", lambda e: e.tensor_tensor(out=scr_f[fo][:], in0=scr_f[fo][:], in1=scr_f[fm][:], op=ALU.add), r=[scr_f_t[fo], scr_f_t[fm]], w=[scr_f_t[fo]])
                    P.op("dve", lambda e: e.scalar_tensor_tensor(out=stt[d][:], in0=scr_f[fo][:], scalar=m2, in1=stt[d][:], op0=ALU.mult, op1=ALU.add), r=[scr_f_t[fo], stt_t[d], msk_t], w=[stt_t[d]])

                def finish(i, fy):
                    bA, bAt = bank()
                    bB, bBt = bank()
                    for c in range(8):
                        bb, bbt = (bA, bAt) if c < 4 else (bB, bBt)
                        P.op("pe", lambda e: e.transpose(out=ps[:, bb, (c % 4) * 128:(c % 4 + 1) * 128], in_=scr_f[fy][:, c * 128:(c + 1) * 128], identity=ident_f), r=[scr_f_t[fy], cst_t], w=[bbt])
                    P.dma("sp", ztl[:], z_d[:, :, i * 128:(i + 1) * 128], r=[zd_t], w=[ztl_t])
                    P.op("act", lambda e: e.activation(out=szt[:], in_=ztl[:], func=AF.Silu), r=[ztl_t], w=[szt_t])
                    P.op("dve", lambda e: e.tensor_tensor(out=gT[:, 0:4, :].rearrange("p c t -> p (c t)"), in0=ps[:, bA, :], in1=szt[:, 0:4, :].rearrange("p c t -> p (c t)"), op=ALU.mult), r=[bAt, szt_t], w=[gT_t])
                    P.op("dve", lambda e: e.tensor_tensor(out=gT[:, 4:8, :].rearrange("p c t -> p (c t)"), in0=ps[:, bB, :], in1=szt[:, 4:8, :].rearrange("p c t -> p (c t)"), op=ALU.mult), r=[bBt, szt_t, gT_t], w=[gT_t])
                    P.op("act", lambda e: e.activation(out=gsq[:], in_=gT[:], func=AF.Square), r=[gT_t], w=[gsq_t])
                    b, bt = bank()
                    for c in range(8):
                        P.op("pe", lambda e: e.matmul(ps[:, b, 0:128], ones_b, gsq[:, c, :], start=(c == 0), stop=(c == 7)), r=[gsq_t, cstb_t], w=[bt])
                    P.op("act", lambda e: e.activation(out=grs[:], in_=ps[:, b, 0:128], func=AF.Sqrt, scale=1.0 / 1024, bias=EPS), r=[bt], w=[grs_t])
                    P.op("dve", lambda e: e.reciprocal(out=grs[:], in_=grs[:]), r=[grs_t], w=[grs_t])
                    P.op("dve", lambda e: e.tensor_tensor(out=gT[:], in0=gT[:], in1=grs[:].unsqueeze(1).to_broadcast([128, 8, 128]), op=ALU.mult), r=[grs_t, gT_t], w=[gT_t])
                    P.op("pool", lambda e: e.tensor_tensor(out=hT[:, 8:16, i * 128:(i + 1) * 128], in0=gT[:], in1=snw[:, l, :].unsqueeze(2).to_broadcast([128, 8, 128]), op=ALU.mult), r=[gT_t, par_t], w=[hT_t])

                for si, seg in enumerate(segs):
                    for d in range(2):
                        order = seg if d == 0 else seg[::-1]
                        incoming_state(si, d)
                        for i in order:
                            bs = chunk_pre(i, d)
                            P.op("act", lambda e: e.activation(out=prevb[d][:], in_=stt[d][:], func=AF.Copy), r=[stt_t[d]], w=[prevb_t[d]])
                            for g in range(2):
                                b, bt = bank()
                                P.op("pe", lambda e: e.matmul(ps[:, b, 0:128], bcT[:, g, i * 128:(i + 1) * 128], bcT[:, 2 + g, i * 128:(i + 1) * 128], start=True, stop=True), r=[bcT_t[g], bcT_t[2 + g]], w=[bt])
                                P.op("dve", lambda e: e.tensor_tensor(out=cbm[:, g, :], in0=ps[:, b, 0:128], in1=tri_f32[d], op=ALU.mult), r=[bt, cst_t], w=[cbm_t])
                            byb = [lbank(), lbank()]
                            for h in range(16):
                                g = h // 8
                                hh = hd[h % 2]
                                ht = hd_t[h % 2]
                                b, bt = bank()
                                P.op("pe", lambda e: e.matmul(ps[:, b, 0:128], selh[:, h * 128:(h + 1) * 128], cs_fm[:], start=True, stop=True), r=[selh_t, pre_t], w=[bt])
                                P.op("dve", lambda e: e.tensor_scalar(out=hh["diff"][:], in0=ps[:, b, 0:128], scalar1=cs_tok[:, 0, h:h + 1], scalar2=0.0, op0=ALU.subtract, op1=ALU.min), r=[bt, pre_t], w=[ht["diff"]])
                                P.op("act", lambda e: e.activation(out=hh["E"][:], in_=hh["diff"][:], func=AF.Exp), r=[ht["diff"]], w=[ht["E"]])
                                P.op("dve", lambda e: e.scalar_tensor_tensor(out=hh["MT"][:], in0=hh["E"][:], scalar=dt_tok[:, i, d * 16 + h:d * 16 + h + 1], in1=cbm[:, g, :], op0=ALU.mult, op1=ALU.mult), r=[ht["E"], dtk_t, cbm_t], w=[ht["MT"]])
                                P.op("act", lambda e: e.activation(out=hh["EB"][:], in_=ps[:, b, 0:128], func=AF.Exp), r=[bt], w=[ht["EB"]])
                                P.op("pool", lambda e: e.tensor_tensor(out=hh["Cs"][:], in0=bcT[:, 2 + g, i * 128:(i + 1) * 128], in1=hh["EB"][:], op=ALU.mult), r=[ht["EB"], bcT_t[2 + g]], w=[ht["Cs"]])
                                yb, ybt = byb[h // 8]
                                co = (h % 8) * 64
                                P.op("pe", lambda e: e.matmul(ps[:, yb, co:co + 64], hh["MT"][:], xtok[:, i, h * 64:(h + 1) * 64], start=True, stop=False), r=[ht["MT"], xtok_t[i]], w=[ybt])
                                P.op("pe", lambda e: e.matmul(ps[:, yb, co:co + 64], hh["Cs"][:], prevb[d][:, h * 64:(h + 1) * 64], start=False, stop=True), r=[ht["Cs"], prevb_t[d]], w=[ybt])
                            fy = nxt("f", 4)
                            if d == 0:
                                P.op("pool", lambda e: e.tensor_tensor(out=scr_f[fy][:], in0=xtok[:, i, :], in1=dexp[:], op=ALU.mult), r=[xtok_t[i], dexp_t], w=[scr_f_t[fy]])
                            else:
                                P.dma("sp", scr_f[fy][:], y_d[i], r=[yd_t[i]], w=[scr_f_t[fy]])
                            for g in range(2):
                                yb, ybt = byb[g]
                                P.op("dve", lambda e: e.tensor_tensor(out=scr_f[fy][:, g * 512:(g + 1) * 512], in0=scr_f[fy][:, g * 512:(g + 1) * 512], in1=ps[:, yb, :], op=ALU.add), r=[ybt, scr_f_t[fy]], w=[scr_f_t[fy]])
                            if d == 0:
                                P.dma("sp", y_d[i], scr_f[fy][:], r=[scr_f_t[fy]], w=[yd_t[i]])
                            elif need_ctx or i > 0:
                                finish(i, fy)
                            state_update(d, bs)
                if "mix" in dbg and l == 0:
                    dump("mix", attT, [att_t], 8, T)
                    dump("mix2", hT[:, 8:16, :], [hT_t], 8, T)
                P.phase_end()
            if stop_after == "B":
                return nc

            with ExitStack() as ph:
                sbp = mk_sb(ph)
                env = dict(upre=[sbp(f"upre{i}", [128, TH]) for i in range(2)], upre_t=P.tiles(2, "upre"),
                           cacc=[sbp(f"cacc{i}", [128, T]) for i in range(2)], cacc_t=P.tiles(2, "cacc"))
                wbuf = [sbp(f"wbufc{i}", [128, KC, 256], BF16) for i in range(4)]
                wbuf_t = P.tiles(4, "wbufc")
                aT = sbp("aT", [128, 12, T], BF16)
                aT_t = P.tiles(12, "aT")
                wdn = [sbp(f"wdn{i}", [128, 12, 256], BF16) for i in range(2)]
                wdn_t = P.tiles(2, "wdn")
                gact = sbp("gact", [128, T])
                gact_t = P.tile("gact")

                def resid_update(b, bt, c, gi, m_g):
                    o, n = TT[gi]
                    r_ = 1 if gi == 0 else 0
                    f = nxt("f", 4)
                    P.dma("sp", scr_f[f][:, 0:n], xs_d[:, c, o:o + n], r=[xs_t[c][gi]], w=[scr_f_t[f]])
                    P.op("dve", lambda e: e.scalar_tensor_tensor(out=scr_f[f][:, 0:n], in0=ps[:, b, 0:n], scalar=mod(l, m_g, c, r_), in1=scr_f[f][:, 0:n], op0=ALU.mult, op1=ALU.add), r=[bt, modT_t, scr_f_t[f]], w=[scr_f_t[f]])
                    P.dma("sp", xs_d[:, c, o:o + n], scr_f[f][:, 0:n], r=[scr_f_t[f]], w=[xs_t[c][gi]])

                for blk in range(8):
                    wi = nxt("wc", 4)
                    wb, wt = wbuf[wi], wbuf_t[wi]
                    P.dma("pool", wb[:], wview(w_out_d[l], blk * 256, 256), w=[wt])
                    for oc in range(2):
                        c = blk * 2 + oc
                        for gi, (o, n) in enumerate(TT):
                            if not need_ctx and gi == 0:
                                continue
                            b, bt = bank()
                            for kc in range(KC):
                                src, srct = (attT[:, kc, o:o + n], att_t) if kc < 8 else (hT[:, kc, o:o + n], hT_t)
                                P.op("pe", lambda e: e.matmul(ps[:, b, 0:n], wb[:, kc, oc * 128:(oc + 1) * 128], src, start=(kc == 0), stop=(kc == KC - 1)), r=[wt, srct], w=[bt])
                            resid_update(b, bt, c, gi, 2)
                if "xmid" in dbg and l == 0:
                    for c in range(KC):
                        P.dma("sp", dbg_d["xmid"][:, c, :], xs_d[:, c, :], r=xs_t[c], w=[])
                rms_modulate(l, nfw, 4, 3)
                edge_exchange()
                GS = [12, 12, 10, 10]
                j0 = 0
                w_up_l = w_up_d[l]
                for G in GS:
                    for jp in range(G // 2):
                        ja = j0 + jp * 2
                        wig = nxt("wc", 4)
                        P.dma("pool", wbuf[wig][:], wview(w_up_l, ja * 128, 256), w=[wbuf_t[wig]])
                        wiv = nxt("wc", 4)
                        P.dma("pool", wbuf[wiv][:], wview(w_up_l, DFF + ja * 128, 256), w=[wbuf_t[wiv]])
                        for jj in range(2):
                            j = ja + jj
                            jg = j - j0
                            ui_g = proj_fm_conv(env, wbuf[wig], wbuf_t[wig], jj, need_ctx)
                            ci_g = conv3(env, ui_g, [fcw[:, l, k, j:j + 1] for k in range(3)], need_ctx)
                            c0_ = 0 if need_ctx else 128
                            P.op("act", lambda e: e.activation(out=gact[:, c0_:T], in_=env["cacc"][ci_g][:, c0_:T], func=AF.Silu, bias=fcb[:, l, j:j + 1]), r=[env["cacc_t"][ci_g], par_t], w=[gact_t])
                            ui_v = proj_fm_conv(env, wbuf[wiv], wbuf_t[wiv], jj, need_ctx)
                            ci_v = conv3(env, ui_v, [fcw[:, l, k, NJ + j:NJ + j + 1] for k in range(3)], need_ctx)
                            P.op("dve", lambda e: e.scalar_tensor_tensor(out=aT[:, jg, c0_:T], in0=env["cacc"][ci_v][:, c0_:T], scalar=fcb[:, l, NJ + j:NJ + j + 1], in1=gact[:, c0_:T], op0=ALU.add, op1=ALU.mult), r=[env["cacc_t"][ci_v], gact_t, par_t], w=[aT_t[jg]])
                    for ocp in range(8):
                        wi = nxt("wd", 2)
                        P.dma("pool", wdn[wi][:, 0:G, :], w_dn_d[l][j0 * 128:(j0 + G) * 128, ocp * 256:(ocp + 1) * 256].rearrange("(j p) n -> p j n", p=128), w=[wdn_t[wi]])
                        for oc2 in range(2):
                            c = ocp * 2 + oc2
                            for gi, (o, n) in enumerate(TT):
                                if not need_ctx and gi == 0:
                                    continue
                                b, bt = bank()
                                for jg in range(G):
                                    P.op("pe", lambda e: e.matmul(ps[:, b, 0:n], wdn[wi][:, jg, oc2 * 128:(oc2 + 1) * 128], aT[:, jg, o:o + n], start=(jg == 0), stop=(jg == G - 1)), r=[wdn_t[wi], aT_t[jg]], w=[bt])
                                resid_update(b, bt, c, gi, 5)
                    j0 += G
                if l == n_layers - 1:
                    out_t = P.tile("out")
                    for c in range(KC):
                        P.dma("sp", yT_d[:, c, :], xs_d[:, c, 128:T], r=xs_t[c], w=[out_t])
                    if "ctxo" in dbg:
                        for c in range(KC):
                            P.dma("sp", dbg_d["ctxo"][:, c, :], xs_d[:, c, 0:128], r=xs_t[c], w=[out_t])
                P.phase_end()
    return nc


def _consts():
    ident = np.eye(128, dtype=np.float32)
    ones = np.ones((128, 128), np.float32)
    s_ = np.arange(128)[:, None]
    l_ = np.arange(128)[None, :]
    trif = (s_ <= l_).astype(np.float32)
    trib = (s_ >= l_).astype(np.float32)
    pm = np.zeros((128, 128), np.float32)
    for base in (0, 64):
        for i in range(32):
            pm[base + 32 + i, base + i] = -1.0
            pm[base + i, base + 32 + i] = 1.0
    cst = np.concatenate([ident, ones, trif, trib, pm], axis=1)
    selh = np.zeros((16, 16 * 128), np.float32)
    for h in range(16):
        selh[h, h * 128:(h + 1) * 128] = 1.0
    return cst, selh


def _rope_tables():
    GRID_W, PAIRS_ = 64, 32
    t = np.arange(2048)
    row = (t // GRID_W).astype(np.float32)
    col = (t % GRID_W).astype(np.float32)
    inv = (10000.0 ** (-np.arange(PAIRS_, dtype=np.float32) / PAIRS_)).astype(np.float32)
    ar = row[:, None] * inv[None, :]
    ac = col[:, None] * inv[None, :]
    cos = np.concatenate([np.cos(ar), np.cos(ar), np.cos(ac), np.cos(ac)], axis=1)
    sin = np.concatenate([np.sin(ar), np.sin(ar), np.sin(ac), np.sin(ac)], axis=1)
    return cos.astype(np.float32), sin.astype(np.float32)


PAR_SPEC = [("nmw", (NL, KC)), ("nfw", (NL, KC)), ("qnw", (NL,)), ("knw", (NL,)), ("scw", (NL, 3, 12)), ("scb", (NL, 12)),
            ("snw", (NL, 8)), ("fcw", (NL, 3, 88)), ("fcb", (NL, 88)), ("dtb", (NL, 32)), ("alog", (NL, 32)), ("dsk", (NL, 16))]
PAR_OFF = {}
_o = 0
for _n, _s in PAR_SPEC:
    PAR_OFF[_n] = (_o, _s)
    _o += int(np.prod(_s))
NPAR = _o


def _par_pack(inputs):
    g = lambda k: np.asarray(inputs[k], np.float32)
    fm = lambda a, nch: a.reshape(a.shape[:-1] + (nch, 128))
    out = np.zeros((128, NPAR), np.float32)

    def put(name, arr_p_first):
        o, shp = PAR_OFF[name]
        out[:, o:o + int(np.prod(shp))] = arr_p_first.reshape(128, -1)
    put("nmw", np.moveaxis(fm(g("norm_mix_w"), KC), -1, 0))
    put("nfw", np.moveaxis(fm(g("norm_mlp_w"), KC), -1, 0))
    put("qnw", g("q_norm_w").T)
    put("knw", g("k_norm_w").T)
    put("scw", np.moveaxis(fm(g("ssd_conv_w"), 12), -1, 0))
    put("scb", np.moveaxis(fm(g("ssd_conv_b"), 12), -1, 0))
    put("snw", np.moveaxis(fm(g("ssd_norm_w"), 8), -1, 0))
    put("fcw", np.moveaxis(fm(g("ffn_conv_w"), 88), -1, 0))
    put("fcb", np.moveaxis(fm(g("ffn_conv_b"), 88), -1, 0))
    put("dtb", np.broadcast_to(g("dt_bias").reshape(1, NL, 32), (128, NL, 32)))
    put("alog", np.broadcast_to(g("a_log").reshape(1, NL, 32), (128, NL, 32)))
    put("dsk", np.broadcast_to(g("d_skip").reshape(1, NL, 16), (128, NL, 16)))
    return out


def _in_maps(inputs, NLW=NL, NCORES=8):
    x = np.asarray(inputs["x"], np.float32)
    ctx = np.asarray(inputs["ctx"], np.float32)
    c = np.asarray(inputs["c"], np.float32)
    c_ctx = np.asarray(inputs["c_ctx"], np.float32)
    cst, selh = _consts()
    cos, sin = _rope_tables()
    c_all = np.concatenate([c, c_ctx[None, :]], axis=0)
    cT = np.ascontiguousarray(c_all.T.reshape(KC, 128, 5).transpose(1, 0, 2))
    shared = {k: np.ascontiguousarray(np.asarray(inputs[k], np.float32)[:NLW]) for k in ["w_in", "w_out", "w_up", "w_down"]}
    shared["par"] = _par_pack(inputs)
    MC = 12288 // NCORES
    w_ada = np.asarray(inputs["w_ada"], np.float32)
    b_ada = np.asarray(inputs["b_ada"], np.float32)
    maps = []
    for core in range(NCORES):
        b, hf = core // 2, core % 2
        tok = np.concatenate([ctx[b, hf * 128:(hf + 1) * 128], x[b, hf * 1024:(hf + 1) * 1024]], axis=0)
        xT = np.ascontiguousarray(tok.T.reshape(KC, 128, T).transpose(1, 0, 2))
        sel = np.zeros((5, 2), np.float32)
        sel[b, 0] = 1.0
        sel[4, 1] = 1.0
        rope = np.ascontiguousarray(np.stack([cos[hf * 1024:(hf + 1) * 1024].T, sin[hf * 1024:(hf + 1) * 1024].T], axis=1))
        msk = np.zeros((128, 2), np.float32)
        msk[:, hf] = 1.0
        selbc = np.ascontiguousarray(np.broadcast_to(sel.T.reshape(1, 2, 5), (128, 2, 5)))
        m = dict(xT=xT, cT=cT, sel=sel, selbc=selbc, cst=cst, selh=selh, rope=rope, msk=msk,
                 w_ada_s=np.ascontiguousarray(w_ada[:NLW, :, core * MC:(core + 1) * MC]),
                 badd=np.ascontiguousarray(np.broadcast_to(b_ada[:NLW, core * MC:(core + 1) * MC].reshape(1, -1), (5, NLW * MC))))
        m.update(shared)
        maps.append(m)
    return maps


def _assemble(results, key="yT", ntok=TLAT):
    out = np.zeros((4, 2 * ntok, D), np.float32)
    for core in range(8):
        b, hf = core // 2, core % 2
        yT = np.asarray(results[core][key])
        out[b, hf * ntok:(hf + 1) * ntok] = yT.transpose(1, 0, 2).reshape(D, ntok).T
    return out


def kernel(**inputs):
    nc = build()
    maps = _in_maps(inputs)
    res = run_bass_kernel_spmd(nc, maps, core_ids=list(range(8)))
    return _assemble(res.results)
```
